# Optimizing a Trainium2 kernel written in Bass

```python
import math
import jax, jax.numpy as jnp
from jax import lax
import numpy as np

D_MODEL = 2048
BATCH = 8
SEQ = 2048
DEPTH = 2

N_MIXERS = 2
D_FF = 256 * math.ceil((8 * D_MODEL / 3) / 256)
M_HEADS = 4
M_DV = D_MODEL // M_HEADS
M_DQK = M_DV // 2
M_CHUNK = 128
D_HEADS = 8
D_HDIM = D_MODEL // (2 * D_HEADS)
Q_BLOCK = 128
DEEPNORM_ALPHA = (2 * DEPTH) ** 0.25
DEEPNORM_BETA = (8 * DEPTH) ** -0.25
N_M_LAYERS = (DEPTH + 1) // 2
N_D_LAYERS = DEPTH // 2
LN_EPS = 1e-5
RMS_EPS = 1e-6

kernel_name = 'hybrid_mlstm_diffattn_macaron_deepnorm'


def _layernorm(x, gain, bias):
    xf = x.astype(jnp.float32)
    mu = jnp.mean(xf, axis=-1, keepdims=True)
    var = jnp.mean(jnp.square(xf - mu), axis=-1, keepdims=True)
    y = (xf - mu) * lax.rsqrt(var + LN_EPS) * gain.astype(jnp.float32) + bias.astype(jnp.float32)
    return y.astype(x.dtype)


def _rmsnorm(x, gain):
    xf = x.astype(jnp.float32)
    y = xf * lax.rsqrt(jnp.mean(jnp.square(xf), axis=-1, keepdims=True) + RMS_EPS)
    return y * gain.astype(jnp.float32)


def _swiglu(x, w_in, w_out):
    gu = x @ w_in
    g, u = gu[..., :D_FF], gu[..., D_FF:]
    return (jax.nn.silu(g) * u) @ w_out


def _alibi_slopes(n_heads):
    return jnp.asarray([2.0 ** (-8.0 * (h + 1) / n_heads) for h in range(n_heads)], jnp.float32)


def _mlstm_chunkwise(q, k, v, i_pre, logf):
    B, S, H, DK = q.shape
    DV = v.shape[-1]
    nc = S // M_CHUNK

    def to_chunks(a):
        a = a.astype(jnp.float32).reshape((B, nc, M_CHUNK, H) + a.shape[3:])
        return jnp.moveaxis(a, (1, 3), (0, 2))

    qc, kc, vc = to_chunks(q), to_chunks(k), to_chunks(v)
    ic, fc = to_chunks(i_pre), to_chunks(logf)
    tri = jnp.tril(jnp.ones((M_CHUNK, M_CHUNK), dtype=bool))

    def step(carry, inp):
        C, n, m = carry
        qb, kb, vb, ib, fb = inp
        b = jnp.cumsum(fb, axis=-1)
        dmat = b[..., :, None] - b[..., None, :] + ib[..., None, :]
        dmat = jnp.where(tri, dmat, -jnp.inf)
        m_inter = b + m[..., None]
        m_t = jnp.maximum(m_inter, jnp.max(dmat, axis=-1))
        w = jnp.einsum('bhtd,bhsd->bhts', qb, kb) * jnp.exp(dmat - m_t[..., None])
        inter = jnp.exp(m_inter - m_t)
        num = inter[..., None] * jnp.einsum('bhtd,bhde->bhte', qb, C) + jnp.einsum('bhts,bhse->bhte', w, vb)
        den = inter * jnp.einsum('bhtd,bhd->bht', qb, n) + jnp.sum(w, axis=-1)
        h = num / jnp.maximum(jnp.abs(den), jnp.exp(-m_t))[..., None]
        b_last = b[..., -1]
        g = b_last[..., None] - b + ib
        m_new = jnp.maximum(b_last + m, jnp.max(g, axis=-1))
        decay = jnp.exp(b_last + m - m_new)
        ws = jnp.exp(g - m_new[..., None])
        C_new = decay[..., None, None] * C + jnp.einsum('bhs,bhsd,bhse->bhde', ws, kb, vb)
        n_new = decay[..., None] * n + jnp.einsum('bhs,bhsd->bhd', ws, kb)
        return (C_new, n_new, m_new), h

    init = (jnp.zeros((B, H, DK, DV), jnp.float32), jnp.zeros((B, H, DK), jnp.float32),
            jnp.zeros((B, H), jnp.float32))
    _, hc = lax.scan(step, init, (qc, kc, vc, ic, fc))
    return jnp.moveaxis(hc, (0, 2), (1, 3)).reshape(B, S, H, DV)


def _mlstm_mixer(x, w_in, b_gates, norm_gain, w_out):
    B, S, _ = x.shape
    qk_w = M_HEADS * M_DQK
    v_w = M_HEADS * M_DV
    proj = x @ w_in
    q = proj[..., :qk_w].reshape(B, S, M_HEADS, M_DQK)
    k = proj[..., qk_w:2 * qk_w].reshape(B, S, M_HEADS, M_DQK) * (M_DQK ** -0.5)
    v = proj[..., 2 * qk_w:2 * qk_w + v_w].reshape(B, S, M_HEADS, M_DV)
    og = proj[..., 2 * qk_w + v_w:2 * qk_w + 2 * v_w].reshape(B, S, M_HEADS, M_DV)
    gates = (proj[..., 2 * qk_w + 2 * v_w:] + b_gates).astype(jnp.float32)
    i_pre = gates[..., :M_HEADS]
    logf = jax.nn.log_sigmoid(gates[..., M_HEADS:])
    h = _mlstm_chunkwise(q, k, v, i_pre, logf)
    h = _rmsnorm(h, norm_gain).astype(x.dtype) * jax.nn.sigmoid(og)
    return h.reshape(B, S, v_w) @ w_out


def _diff_attention_mixer(x, w_in, lam_vecs, norm_gain, w_out, layer_idx):
    B, S, _ = x.shape
    qd = D_HEADS * 2 * D_HDIM
    proj = x @ w_in
    q = proj[..., :qd].reshape(B, S, D_HEADS, 2, D_HDIM)
    k = proj[..., qd:2 * qd].reshape(B, S, D_HEADS, 2, D_HDIM)
    v = proj[..., 2 * qd:].reshape(B, S, D_HEADS, 2 * D_HDIM)
    lambda_init = 0.8 - 0.6 * math.exp(-0.3 * layer_idx)
    lv = lam_vecs.astype(jnp.float32)
    lam = jnp.exp(jnp.sum(lv[0] * lv[1])) - jnp.exp(jnp.sum(lv[2] * lv[3])) + lambda_init
    slopes = _alibi_slopes(D_HEADS)
    scale = D_HDIM ** -0.5
    nb = S // Q_BLOCK
    qb = q.reshape(B, nb, Q_BLOCK, D_HEADS, 2, D_HDIM).transpose(1, 0, 3, 4, 2, 5)
    kpos = jnp.arange(S)

    def block(args):
        q_blk, blk = args
        s = jnp.einsum('bhcqd,bshcd->bhcqs', q_blk, k).astype(jnp.float32) * scale
        qpos = blk * Q_BLOCK + jnp.arange(Q_BLOCK)
        dist = qpos[:, None] - kpos[None, :]
        bias = -slopes[:, None, None] * dist.astype(jnp.float32)
        s = jnp.where((dist >= 0)[None, None, None], s + bias[None, :, None], -jnp.inf)
        p = jax.nn.softmax(s, axis=-1)
        a = p[:, :, 0] - lam * p[:, :, 1]
        return jnp.einsum('bhqs,bshe->bqhe', a.astype(v.dtype), v)

    out = lax.map(block, (qb, jnp.arange(nb)))
    out = out.transpose(1, 0, 2, 3, 4).reshape(B, S, D_HEADS, 2 * D_HDIM)
    out = (_rmsnorm(out, norm_gain) * (1.0 - lambda_init)).astype(x.dtype)
    return out.reshape(B, S, qd) @ w_out


def _normal(key, shape, scale):
    return jax.random.normal(key, shape, jnp.float32) * scale


def setup_inputs(seed: int = 0) -> dict:
    key = jax.random.key(seed)
    ks = jax.random.split(key, 20)
    fi = D_MODEL ** -0.5
    x = _normal(ks[0], (BATCH, SEQ, D_MODEL), 1.0)
    ffn1_w_in = _normal(ks[1], (DEPTH, D_MODEL, 2 * D_FF), fi * DEEPNORM_BETA)
    ffn1_w_out = _normal(ks[2], (DEPTH, D_FF, D_MODEL), D_FF ** -0.5 * DEEPNORM_BETA)
    ffn2_w_in = _normal(ks[3], (DEPTH, D_MODEL, 2 * D_FF), fi * DEEPNORM_BETA)
    ffn2_w_out = _normal(ks[4], (DEPTH, D_FF, D_MODEL), D_FF ** -0.5 * DEEPNORM_BETA)
    ln_gain = 1.0 + _normal(ks[5], (DEPTH, 3, D_MODEL), 0.02)
    ln_bias = _normal(ks[6], (DEPTH, 3, D_MODEL), 0.02)
    m_w_qk = _normal(ks[7], (N_M_LAYERS, D_MODEL, 2 * M_HEADS * M_DQK), fi)
    m_w_v = _normal(ks[8], (N_M_LAYERS, D_MODEL, M_HEADS * M_DV), fi * DEEPNORM_BETA)
    m_w_og = _normal(ks[9], (N_M_LAYERS, D_MODEL, M_HEADS * M_DV), fi)
    m_w_if = _normal(ks[10], (N_M_LAYERS, D_MODEL, 2 * M_HEADS), fi * 0.1)
    m_w_in = jnp.concatenate([m_w_qk, m_w_v, m_w_og, m_w_if], axis=-1)
    m_b_i = _normal(ks[11], (N_M_LAYERS, M_HEADS), 0.1)
    m_b_f = jnp.linspace(3.0, 6.0, M_HEADS)[None, :] + _normal(ks[12], (N_M_LAYERS, M_HEADS), 0.1)
    m_b_gates = jnp.concatenate([m_b_i, m_b_f], axis=-1)
    m_norm_gain = 1.0 + _normal(ks[13], (N_M_LAYERS, M_HEADS, M_DV), 0.02)
    m_w_out = _normal(ks[14], (N_M_LAYERS, M_HEADS * M_DV, D_MODEL), (M_HEADS * M_DV) ** -0.5 * DEEPNORM_BETA)
    d_w_qk = _normal(ks[15], (N_D_LAYERS, D_MODEL, 4 * D_HEADS * D_HDIM), fi)
    d_w_v = _normal(ks[16], (N_D_LAYERS, D_MODEL, 2 * D_HEADS * D_HDIM), fi * DEEPNORM_BETA)
    d_w_in = jnp.concatenate([d_w_qk, d_w_v], axis=-1)
    d_lambda = _normal(ks[17], (N_D_LAYERS, 4, D_HDIM), 0.1)
    d_norm_gain = 1.0 + _normal(ks[18], (N_D_LAYERS, D_HEADS, 2 * D_HDIM), 0.02)
    d_w_out = _normal(ks[19], (N_D_LAYERS, 2 * D_HEADS * D_HDIM, D_MODEL), (2 * D_HEADS * D_HDIM) ** -0.5 * DEEPNORM_BETA)
    return {'x': x, 'ffn1_w_in': ffn1_w_in, 'ffn1_w_out': ffn1_w_out, 'ffn2_w_in': ffn2_w_in,
            'ffn2_w_out': ffn2_w_out, 'ln_gain': ln_gain, 'ln_bias': ln_bias, 'm_w_in': m_w_in,
            'm_b_gates': m_b_gates, 'm_norm_gain': m_norm_gain, 'm_w_out': m_w_out, 'd_w_in': d_w_in,
            'd_lambda': d_lambda, 'd_norm_gain': d_norm_gain, 'd_w_out': d_w_out}


def reference(x, ffn1_w_in, ffn1_w_out, ffn2_w_in, ffn2_w_out, ln_gain, ln_bias, m_w_in, m_b_gates,
              m_norm_gain, m_w_out, d_w_in, d_lambda, d_norm_gain, d_w_out):
    for i in range(DEPTH):
        x = _layernorm(DEEPNORM_ALPHA * x + 0.5 * _swiglu(x, ffn1_w_in[i], ffn1_w_out[i]), ln_gain[i, 0], ln_bias[i, 0])
        j = i // N_MIXERS
        if i % N_MIXERS == 0:
            mix = _mlstm_mixer(x, m_w_in[j], m_b_gates[j], m_norm_gain[j], m_w_out[j])
        else:
            mix = _diff_attention_mixer(x, d_w_in[j], d_lambda[j], d_norm_gain[j], d_w_out[j], i)
        x = _layernorm(DEEPNORM_ALPHA * x + mix, ln_gain[i, 1], ln_bias[i, 1])
        x = _layernorm(DEEPNORM_ALPHA * x + 0.5 * _swiglu(x, ffn2_w_in[i], ffn2_w_out[i]), ln_gain[i, 2], ln_bias[i, 2])
    return x
```

```python
import math
import os
import numpy as np
import concourse.bass as bass
import concourse.mybir as mybir
from concourse.bass_utils import run_bass_kernel_spmd

F32 = mybir.dt.float32
BF16 = mybir.dt.bfloat16
AF = mybir.ActivationFunctionType
ALU = mybir.AluOpType
AX = mybir.AxisListType

D = 2048
S = 2048
DFF = 5632
NCH = D // 128
NFF = DFF // 128
TB = 512
NTB = S // TB
DEPTH = 2
ALPHA = (2 * DEPTH) ** 0.25
LN_EPS = 1e-5
RMS_EPS = 1e-6
M_HEADS, M_DV, M_DQK = 4, 512, 256
D_HEADS, D_HDIM = 8, 128


class _Op:
    __slots__ = ("idx", "eng", "fn", "dma", "deps", "signalled", "sig", "sem")

    def __init__(self, idx, eng, fn, dma):
        self.idx, self.eng, self.fn, self.dma = idx, eng, fn, dma
        self.deps = set()
        self.signalled = False
        self.sig = 0
        self.sem = None


class Sched:
    COMPUTE = ("pe", "act", "dve", "pool")

    def __init__(self, nc):
        self.nc = nc
        self.ops = []
        self.last_writer = {}
        self.readers = {}
        self.pending = {}

    def barrier(self):
        last = {}
        for op in self.ops:
            last[op.eng] = op
        for eng in ("pe", "act", "dve", "pool", "sp"):
            self.pending[eng] = set(last.values()) | self.pending.get(eng, set())

    def add(self, eng, fn, reads=(), writes=(), dma=None):
        op = _Op(len(self.ops), eng, fn, dma)
        deps = set()
        for r in reads:
            w = self.last_writer.get(r)
            if w is not None:
                deps.add(w)
        for k in writes:
            w = self.last_writer.get(k)
            if w is not None:
                deps.add(w)
            for rd in self.readers.get(k, {}).values():
                deps.add(rd)
        for r in reads:
            d = self.readers.setdefault(r, {})
            d[(eng, op.idx) if dma is not None else eng] = op
        for k in writes:
            self.last_writer[k] = op
            self.readers[k] = {}
        if self.pending.get(eng):
            deps |= self.pending[eng]
            self.pending[eng] = set()
        deps.discard(op)
        op.deps = {d for d in deps if not (d.eng == "pe" and eng == "pe" and d.dma is None and dma is None)}
        for d in op.deps:
            d.signalled = True
        if dma is not None:
            op.signalled = True
        self.ops.append(op)
        return op

    def emit(self, final_waits=()):
        nc = self.nc
        sems = {}
        cnt = {}

        def get_sem(key):
            if key not in sems:
                sems[key] = nc.alloc_semaphore("s_" + "_".join(str(k) for k in (key if isinstance(key, tuple) else (key,))))
                cnt[key] = 0
            return sems[key]

        for op in self.ops:
            if not op.signalled:
                continue
            key = ("dma",) + tuple(op.dma if isinstance(op.dma, tuple) else (op.dma,)) if op.dma is not None else ("eng", op.eng)
            op.sem = get_sem(key)
            cnt[key] += 16 if op.dma is not None else 1
            op.sig = cnt[key]
        by_eng = {}
        for op in self.ops:
            by_eng.setdefault(op.eng, []).append(op)

        def run(engname, e):
            waited = {}
            for op in by_eng.get(engname, []):
                need = {}
                for d in op.deps:
                    s = d.sem
                    if need.get(s, 0) < d.sig:
                        need[s] = d.sig
                for s, v in need.items():
                    if waited.get(s, 0) < v:
                        e.wait_ge(s, v)
                        waited[s] = v
                ins = op.fn(e)
                if op.signalled:
                    ins.then_inc(op.sem, 16 if op.dma is not None else 1)
            for d in final_waits:
                if engname == "sp":
                    e.wait_ge(d.sem, d.sig)

        with nc.Block() as block:
            @block.sync
            def _(e):
                run("sp", e)

            @block.tensor
            def _(e):
                run("pe", e)

            @block.scalar
            def _(e):
                run("act", e)

            @block.vector
            def _(e):
                run("dve", e)

            @block.gpsimd
            def _(e):
                run("pool", e)


class Builder:
    def __init__(self, phases, nblocks=NTB, debug=False):
        self.phases = phases
        self.nblocks = nblocks
        self.debug = debug
        nc = self.nc = bass.Bass("TRN2", target_bir_lowering=False)
        self.S = Sched(nc)
        dt = nc.dram_tensor
        kinds = {p[0] for p in phases}
        self.d_x = dt("xT_in", [128, NCH, S], F32, kind="ExternalInput").ap()
        self.d_lng = dt("ln_g", [128, 6, NCH], F32, kind="ExternalInput").ap()
        self.d_lnb = dt("ln_b", [128, 6, NCH], F32, kind="ExternalInput").ap()
        self.d_cf = dt("c_f32", [128, 3, 128], F32, kind="ExternalInput").ap()
        self.d_cb = dt("c_bf", [128, 2, 128], F32, kind="ExternalInput").ap()
        if "ffn" in kinds:
            self.d_win = dt("ffn_win", [4, NFF, 128, NCH, 256], F32, kind="ExternalInput").ap()
            self.d_wout = dt("ffn_wout", [4, NCH, 128, NFF, 128], F32, kind="ExternalInput").ap()
        if "mlstm" in kinds:
            self.d_mw = dt("m_w", [M_HEADS, 6, 128, NCH, 256], F32, kind="ExternalInput").ap()
            self.d_mwif = dt("m_wif", [128, NCH, 8], F32, kind="ExternalInput").ap()
            self.d_mbg = dt("m_bg", [128, 32], F32, kind="ExternalInput").ap()
            self.d_mgain = dt("m_gain", [128, NCH], F32, kind="ExternalInput").ap()
            self.d_mwo = dt("m_wo", [NCH, 128, NCH, 128], F32, kind="ExternalInput").ap()
        if "attn" in kinds:
            self.d_dw = dt("d_w", [D_HEADS, 3, 128, NCH, 256], F32, kind="ExternalInput").ap()
            self.d_dwo = dt("d_wo", [NCH, 128, NCH, 128], F32, kind="ExternalInput").ap()
            self.d_dgain = dt("d_gain", [128, NCH], F32, kind="ExternalInput").ap()
            self.d_dlam = dt("d_lam", [128, 512], F32, kind="ExternalInput").ap()
            self.d_alA = dt("al_A", [3, D_HEADS, 128], F32, kind="ExternalInput").ap()
            self.d_alB = dt("al_B", [3, D_HEADS, 512], F32, kind="ExternalInput").ap()
            self.d_alC = dt("al_C", [128, D_HEADS * 20], F32, kind="ExternalInput").ap()
        self.d_out = dt("yT_out", [128, NCH, S], F32, kind="ExternalOutput").ap()

        a = nc.alloc_sbuf_tensor
        self.xT = a("xT", [128, NCH, S], BF16)
        self.lng = a("lng", [128, 6, NCH], F32)
        self.lnb = a("lnb", [128, 6, NCH], F32)
        self.ones_f = a("ones_f", [128, 128], F32)
        self.cf = a("cf", [128, 3, 128], F32)
        self.cb = a("cb", [128, 2, 128], BF16)
        self.ones_col = a("ones_col", [128, 1], BF16)
        a0 = nc._sbuf_addr_for_side(None)
        self.arena0 = (a0 + 63) // 64 * 64
        self.arena_sz = nc.sbuf_bytes_remaining - (self.arena0 - a0)
        self.ps = [nc.alloc_psum_tensor(f"ps{i}", [128, 512], F32) for i in range(7)]
        self.psb = nc.alloc_psum_tensor("ps7b", [128, 1024], BF16)
        self.out_dmas = []
        self.cnt = {}
        self.nview = 0
        self.lay = {}

    def layout(self, name, spec):
        if name in self.lay:
            return self.lay[name]
        off = self.arena0
        out = {}
        for key, shape, dtype in spec:
            nbytes = int(np.prod(shape[1:])) * (2 if dtype == BF16 else 4)
            nbytes = (nbytes + 31) // 32 * 32
            self.nview += 1
            out[key] = self.nc.alloc_sbuf_tensor_at(f"v{self.nview}_{key}", list(shape), dtype, offset=off)
            off += nbytes
        used = off - self.arena0
        assert off <= 229376, (name, used, off)
        self.lay[name] = out
        return out

    def rot(self, key, n):
        v = self.cnt.get(key, 0)
        self.cnt[key] = v + 1
        return v % n

    def prologue(self):
        S_ = self.S
        S_.add("sp", lambda e: e.dma_start(out=self.lng[:], in_=self.d_lng), writes=["lng"], dma="lng")
        S_.add("sp", lambda e: e.dma_start(out=self.lnb[:], in_=self.d_lnb), writes=["lnb"], dma="lnb")
        S_.add("sp", lambda e: e.dma_start(out=self.cf[:], in_=self.d_cf), writes=["cf"], dma="cf")
        S_.add("pool", lambda e: e.dma_start(out=self.cb[:], in_=self.d_cb), writes=["cb"], dma="cb")
        S_.add("dve", lambda e: e.memset(self.ones_f[:], 1.0 / D), writes=["ones_f"])
        S_.add("dve", lambda e: e.memset(self.ones_col[:], 1.0), writes=["ones_col"])
        for b in range(self.nblocks):
            S_.add("pool", lambda e, b=b: e.dma_start(out=self.xT[:, :, b * TB:(b + 1) * TB],
                                                      in_=self.d_x[:, :, b * TB:(b + 1) * TB]),
                   writes=[("xT", b)], dma=("xin", b))

    def ffn(self, fidx, lnidx, final, x_from_input=False):
        S_ = self.S
        ps = self.ps
        L = self.layout("ffn", [("hT", [128, NFF, TB], BF16), ("zT", [128, NCH, TB], F32)]
                        + [(f"win{i}", [128, NCH, 256], BF16) for i in range(3)]
                        + [(f"wout{i}", [128, NFF, 128], BF16) for i in range(2)]
                        + [(f"sg{i}", [128, TB], F32) for i in range(2)]
                        + [(f"sq{i}", [128, TB], F32) for i in range(2)]
                        + [("mean", [128, TB], F32), ("rstd", [128, TB], F32), ("zsum", [128, TB], F32), ("sqsum", [128, TB], F32)])
        hT, zT = L["hT"], L["zT"]
        cscale = 0.5 / ALPHA
        eps = LN_EPS / (ALPHA * ALPHA)
        pending = []
        for b in range(self.nblocks):
            tsl = slice(b * TB, (b + 1) * TB)
            for c in range(NFF):
                if pending and c >= 2:
                    pending.pop(0)()
                if x_from_input and c == 20:
                    S_.add("sp", lambda e, tsl=tsl: e.dma_start(out=zT[:], in_=self.d_x[:, :, tsl]),
                           writes=[("zT", jj) for jj in range(NCH)], dma=("xres", b % 2))
                ws = self.rot("win", 3)
                wt = L[f"win{ws}"]
                S_.add("pool", lambda e, wt=wt, c=c: e.dma_start(out=wt[:], in_=self.d_win[fidx, c]),
                       writes=[("win", ws)], dma=("w", ws))
                pb = self.rot("gu", 2)
                pg, pu = ps[pb], ps[2 + pb]

                def mm1(e, wt=wt, pg=pg, pu=pu, tsl=tsl):
                    for k in range(NCH):
                        e.matmul(pg[:], wt[:, k, 0:128], self.xT[:, k, tsl], start=(k == 0), stop=(k == NCH - 1))
                    for k in range(NCH):
                        ins = e.matmul(pu[:], wt[:, k, 128:256], self.xT[:, k, tsl], start=(k == 0), stop=(k == NCH - 1))
                    return ins
                S_.add("pe", mm1, reads=[("win", ws), ("xT", b)], writes=[("ps", pb), ("ps", 2 + pb)])
                sgt = L[f"sg{pb}"]
                S_.add("act", lambda e, pg=pg, sgt=sgt: e.activation(out=sgt[:], in_=pg[:], func=AF.Silu),
                       reads=[("ps", pb)], writes=[("sg", pb)])
                S_.add("dve", lambda e, pu=pu, sgt=sgt, c=c: e.tensor_tensor(out=hT[:, c, :], in0=pu[:], in1=sgt[:], op=ALU.mult),
                       reads=[("ps", 2 + pb), ("sg", pb)], writes=[("hT", c)])
            for j in range(NCH):
                ws = self.rot("wout", 2)
                wt = L[f"wout{ws}"]
                S_.add("pool", lambda e, wt=wt, j=j: e.dma_start(out=wt[:], in_=self.d_wout[fidx, j]),
                       writes=[("wout", ws)], dma=("wo", ws))
                pb = 4 + self.rot("po", 2)
                po = ps[pb]

                def mm2(e, wt=wt, po=po):
                    for k in range(NFF):
                        ins = e.matmul(po[:], wt[:, k, :], hT[:, k, :], start=(k == 0), stop=(k == NFF - 1))
                    return ins
                S_.add("pe", mm2, reads=[("wout", ws)] + [("hT", c) for c in range(NFF)], writes=[("ps", pb)])
                if x_from_input:
                    S_.add("dve", lambda e, po=po, j=j: e.scalar_tensor_tensor(
                        out=zT[:, j, :], in0=po[:], scalar=cscale, in1=zT[:, j, :], op0=ALU.mult, op1=ALU.add),
                        reads=[("ps", pb), ("zT", j)], writes=[("zT", j)])
                else:
                    S_.add("dve", lambda e, po=po, j=j, tsl=tsl: e.scalar_tensor_tensor(
                        out=zT[:, j, :], in0=po[:], scalar=cscale, in1=self.xT[:, j, tsl], op0=ALU.mult, op1=ALU.add),
                        reads=[("ps", pb), ("xT", b)], writes=[("zT", j)])
                self.z_stats(L, j)
            pending = self.layernorm(L, b, lnidx, eps, final, 4, 5)
        for th in pending:
            th()

    def z_stats(self, L, j):
        S_ = self.S
        zT, zsum, sqsum = L["zT"], L["zsum"], L["sqsum"]
        if j == 0:
            S_.add("act", lambda e: e.activation(out=sqsum[:], in_=zT[:, 0, :], func=AF.Square),
                   reads=[("zT", 0)], writes=["sqsum"])
            return
        sb = self.rot("sq", 2)
        sqt = L[f"sq{sb}"]
        S_.add("act", lambda e, sqt=sqt, j=j: e.activation(out=sqt[:], in_=zT[:, j, :], func=AF.Square),
               reads=[("zT", j)], writes=[("sq", sb)])
        S_.add("dve", lambda e, sqt=sqt: e.tensor_tensor(out=sqsum[:], in0=sqsum[:], in1=sqt[:], op=ALU.add),
               reads=["sqsum", ("sq", sb)], writes=["sqsum"])
        if j == 1:
            S_.add("dve", lambda e: e.tensor_tensor(out=zsum[:], in0=zT[:, 0, :], in1=zT[:, 1, :], op=ALU.add),
                   reads=[("zT", 0), ("zT", 1)], writes=["zsum"])
        else:
            S_.add("dve", lambda e, j=j: e.tensor_tensor(out=zsum[:], in0=zsum[:], in1=zT[:, j, :], op=ALU.add),
                   reads=["zsum", ("zT", j)], writes=["zsum"])

    def layernorm(self, L, b, lnidx, eps, final, bm, bq):
        S_ = self.S
        zT, mean, rstd = L["zT"], L["mean"], L["rstd"]
        tsl = slice(b * TB, (b + 1) * TB)
        pm, pq = self.ps[bm], self.ps[bq]

        def st(e):
            e.matmul(pm[:], self.ones_f[:], L["zsum"][:], start=True, stop=True)
            return e.matmul(pq[:], self.ones_f[:], L["sqsum"][:], start=True, stop=True)
        S_.add("pe", st, reads=["zsum", "sqsum", "ones_f"], writes=[("ps", bm), ("ps", bq)])
        S_.add("act", lambda e: e.activation(out=mean[:], in_=pm[:], func=AF.Copy), reads=[("ps", bm)], writes=["mean"])
        S_.add("dve", lambda e: e.tensor_tensor(out=rstd[:], in0=mean[:], in1=mean[:], op=ALU.mult),
               reads=["mean"], writes=["rstd"])
        S_.add("dve", lambda e: e.tensor_tensor(out=rstd[:], in0=pq[:], in1=rstd[:], op=ALU.subtract),
               reads=[("ps", bq), "rstd"], writes=["rstd"])
        S_.add("dve", lambda e: e.tensor_scalar(out=rstd[:], in0=rstd[:], scalar1=eps, scalar2=None, op0=ALU.add),
               reads=["rstd"], writes=["rstd"])
        S_.add("act", lambda e: e.activation(out=rstd[:], in_=rstd[:], func=AF.Sqrt), reads=["rstd"], writes=["rstd"])
        S_.add("dve", lambda e: e.reciprocal(out=rstd[:], in_=rstd[:]), reads=["rstd"], writes=["rstd"])
        thunks = []

        def norm_j(j):
            S_.add("dve", lambda e: e.tensor_tensor(out=zT[:, j, :], in0=zT[:, j, :], in1=mean[:], op=ALU.subtract),
                   reads=[("zT", j), "mean"], writes=[("zT", j)])
            S_.add("dve", lambda e: e.tensor_tensor(out=zT[:, j, :], in0=zT[:, j, :], in1=rstd[:], op=ALU.mult),
                   reads=[("zT", j), "rstd"], writes=[("zT", j)])
            dst = zT[:, j, :] if final else self.xT[:, j, tsl]
            S_.add("act", lambda e: e.activation(out=dst, in_=zT[:, j, :], func=AF.Identity,
                                                 bias=self.lnb[:, lnidx, j:j + 1], scale=self.lng[:, lnidx, j:j + 1]),
                   reads=[("zT", j), "lng", "lnb"], writes=[("zT", j)] if final else [("xT", b)])
            if final and j == NCH - 1:
                op = S_.add("sp", lambda e: e.dma_start(out=self.d_out[:, :, tsl], in_=zT[:]),
                            reads=[("zT", jj) for jj in range(NCH)], dma=("out", b % 2))
                self.out_dmas.append(op)
        for j in range(NCH):
            thunks.append(lambda j=j: norm_j(j))
        return thunks

    def mlstm(self, lnidx, final):
        S_ = self.S
        ps, psb = self.ps, self.psb
        H = M_HEADS
        L = self.layout("mlstm", [("zT", [128, NCH, TB], F32)]
                        + [(f"sq{i}", [128, TB], F32) for i in range(2)]
                        + [("mean", [128, TB], F32), ("rstd", [128, TB], F32), ("zsum", [128, TB], F32), ("sqsum", [128, TB], F32)]
                        + [(f"w{i}", [128, NCH, 256], BF16) for i in range(3)]
                        + [(f"wo{i}", [128, NCH, 128], BF16) for i in range(2)]
                        + [("wif", [128, NCH, 8], BF16), ("bg", [128, 32], F32), ("gainT", [128, NCH], F32),
                           ("qT", [128, 2, TB], BF16), ("kT", [128, 2, TB], BF16), ("vtok", [128, 4, 512], BF16),
                           ("sgo", [128, 4, TB], BF16), ("ktil", [128, 4, 256], BF16),
                           ("gts", [128, 4, 8], F32), ("e1", [128, 4, 4], F32), ("lf", [128, 16], F32),
                           ("av", [128, 16], F32), ("wsv", [128, 16], F32), ("dec", [128, 16], F32), ("tmp16", [128, 16], F32)]
                        + [(f"lfU{i}", [128, 128], F32) for i in range(2)]
                        + [(f"ebc{i}", [128, 128], F32) for i in range(2)]
                        + [(f"dta{i}", [128, 128], F32) for i in range(2)]
                        + [(f"DT{i}", [128, 128], F32) for i in range(2)]
                        + [(f"qhat{i}", [128, 2, 128], BF16) for i in range(3)]
                        + [(f"wT{i}", [128, 128], BF16) for i in range(3)]
                        + [(f"hn{i}", [128, 512], BF16) for i in range(2)]
                        + [(f"sm{i}", [128, 8], F32) for i in range(2)]
                        + [("C", [128, H, 2, 512], F32), ("Cbf", [128, H, 2, 512], BF16),
                           ("nst", [128, H, 2], F32), ("nbf", [128, H, 2], BF16),
                           ("hgT", [128, NCH, TB], BF16)])
        zT = L["zT"]
        C, Cbf, nst, nbf, hgT = L["C"], L["Cbf"], L["nst"], L["nbf"], L["hgT"]
        qT, kT, vtok, sgo, ktil = L["qT"], L["kT"], L["vtok"], L["sgo"], L["ktil"]
        gts, e1, lf, av, wsv, dec, tmp16 = L["gts"], L["e1"], L["lf"], L["av"], L["wsv"], L["dec"], L["tmp16"]
        U, MB, ONES = self.cf[:, 0, :], self.cf[:, 1, :], self.cf[:, 2, :]
        IDN = self.cb[:, 0, :]
        cscale = 1.0 / ALPHA
        eps = LN_EPS / (ALPHA * ALPHA)
        S_.add("pool", lambda e: e.dma_start(out=L["wif"][:], in_=self.d_mwif), writes=["m_wif"], dma="pc0")
        S_.add("sp", lambda e: e.dma_start(out=L["bg"][:], in_=self.d_mbg), writes=["m_bg"], dma="sc1")
        S_.add("sp", lambda e: e.dma_start(out=L["gainT"][:], in_=self.d_mgain), writes=["m_gain"], dma="sc2")
        S_.add("dve", lambda e: e.memset(C[:], 0.0), writes=[("C", h) for h in range(H)])
        S_.add("dve", lambda e: e.memset(Cbf[:], 0.0), writes=[("Cbf", h) for h in range(H)])
        S_.add("dve", lambda e: e.memset(nst[:], 0.0), writes=[("n", h) for h in range(H)])
        S_.add("dve", lambda e: e.memset(nbf[:], 0.0), writes=[("nbf", h) for h in range(H)])

        def load_w(src):
            ws = self.rot("mw", 3)
            wt = L[f"w{ws}"]
            S_.add("pool", lambda e: e.dma_start(out=wt[:], in_=src), writes=[("mw", ws)], dma=("w", ws))
            return wt, ("mw", ws)

        for b in range(self.nblocks):
            tsl = slice(b * TB, (b + 1) * TB)
            def gmm(e, b=b):
                for tc in range(4):
                    cs = slice(b * TB + tc * 128, b * TB + (tc + 1) * 128)
                    for k in range(NCH):
                        ins = e.matmul(ps[2][:, tc * 8:(tc + 1) * 8], self.xT[:, k, cs], L["wif"][:, k, :],
                                       start=(k == 0), stop=(k == NCH - 1))
                return ins
            S_.add("pe", gmm, reads=[("xT", b), "m_wif"], writes=[("ps", 2)])
            S_.add("dve", lambda e: e.tensor_tensor(out=gts[:].rearrange("p a b -> p (a b)"), in0=ps[2][:, 0:32], in1=L["bg"][:], op=ALU.add),
                   reads=[("ps", 2), "m_bg"], writes=["gts"])
            S_.add("act", lambda e: e.activation(out=e1[:], in_=gts[:, :, 4:8], func=AF.Exp, scale=-1.0), reads=["gts"], writes=["e1"])
            S_.add("act", lambda e: e.activation(out=e1[:], in_=e1[:], func=AF.Ln, bias=1.0), reads=["e1"], writes=["e1"])
            S_.add("dve", lambda e: e.tensor_scalar(out=lf[:], in0=e1[:].rearrange("p a b -> p (a b)"), scalar1=-1.0, scalar2=None, op0=ALU.mult),
                   reads=["e1"], writes=["lf"])

            def bmm(e):
                e.matmul(ps[2][:, 32:48], U, lf[:], start=True, stop=True)
                return e.matmul(ps[2][:, 48:64], ONES, lf[:], start=True, stop=True)
            S_.add("pe", bmm, reads=["lf", "cf"], writes=[("ps", 2)])
            S_.add("dve", lambda e: e.tensor_tensor(out=av[:].rearrange("p (a b) -> p a b", b=4), in0=gts[:, :, 0:4],
                                                    in1=ps[2][:, 32:48].rearrange("p (a b) -> p a b", b=4), op=ALU.subtract),
                   reads=["gts", ("ps", 2)], writes=["av"])
            S_.add("dve", lambda e: e.tensor_tensor(out=tmp16[:], in0=av[:], in1=ps[2][:, 48:64], op=ALU.add),
                   reads=["av", ("ps", 2)], writes=["tmp16"])
            S_.add("act", lambda e: e.activation(out=wsv[:], in_=tmp16[:], func=AF.Exp), reads=["tmp16"], writes=["wsv"])
            S_.add("act", lambda e: e.activation(out=dec[:], in_=ps[2][:, 48:64], func=AF.Exp), reads=[("ps", 2)], writes=["dec"])

            for h in range(H):
                def proj_fm(piece, dst3, func, scale, key, nm=2, moff=0):
                    wt, wkey = load_w(self.d_mw[h, piece])
                    for m in range(nm):
                        pb = self.rot("pj", 2)

                        def mm(e, wt=wt, pb=pb, m=m, tsl=tsl):
                            for k in range(NCH):
                                ins = e.matmul(ps[pb][:], wt[:, k, m * 128:(m + 1) * 128], self.xT[:, k, tsl],
                                               start=(k == 0), stop=(k == NCH - 1))
                            return ins
                        S_.add("pe", mm, reads=[wkey, ("xT", b)], writes=[("ps", pb)])
                        S_.add("act", lambda e, pb=pb, m=m: e.activation(out=dst3[:, moff + m, :], in_=ps[pb][:], func=func, scale=scale),
                               reads=[("ps", pb)], writes=[key])
                proj_fm(0, qT, AF.Copy, 1.0, "qT")
                proj_fm(1, kT, AF.Copy, M_DQK ** -0.5, "kT")
                for half in range(2):
                    wt, wkey = load_w(self.d_mw[h, 2 + half])
                    for tc in range(4):
                        pb = self.rot("pj", 2)

                        def mmv(e, wt=wt, pb=pb, tc=tc, b=b):
                            cs = slice(b * TB + tc * 128, b * TB + (tc + 1) * 128)
                            for k in range(NCH):
                                ins = e.matmul(ps[pb][:, 0:256], self.xT[:, k, cs], wt[:, k, :], start=(k == 0), stop=(k == NCH - 1))
                            return ins
                        S_.add("pe", mmv, reads=[wkey, ("xT", b)], writes=[("ps", pb)])
                        S_.add("act", lambda e, pb=pb, tc=tc, half=half: e.activation(
                            out=vtok[:, tc, half * 256:(half + 1) * 256], in_=ps[pb][:, 0:256], func=AF.Copy),
                            reads=[("ps", pb)], writes=[("vtok", tc)])
                for half in range(2):
                    proj_fm(4 + half, sgo, AF.Sigmoid, 1.0, "sgo", nm=2, moff=half * 2)
                for tc in range(4):
                    col = tc * 4 + h

                    def ktr(e, tc=tc):
                        for m in range(2):
                            ins = e.transpose(psb[:, m * 128:(m + 1) * 128], kT[:, m, tc * 128:(tc + 1) * 128], IDN)
                        return ins
                    S_.add("pe", ktr, reads=["kT", "cb"], writes=[("ps", 7)])
                    S_.add("act", lambda e, tc=tc, col=col: e.activation(out=ktil[:, tc, :], in_=psb[:, 0:256], func=AF.Copy,
                                                                         scale=wsv[:, col:col + 1]),
                           reads=[("ps", 7), "wsv"], writes=[("ktil", tc)])
                def stage_P(tc, h=h):
                    col = tc * 4 + h
                    cs = slice(tc * 128, (tc + 1) * 128)
                    r = self.rot("rrP", 2)
                    rq = self.rot("rrQ", 3)
                    lfU, ebc, dta, DT = L[f"lfU{r}"], L[f"ebc{r}"], L[f"dta{r}"], L[f"DT{r}"]
                    qhat, wT = L[f"qhat{rq}"], L[f"wT{rq}"]
                    bbS = ps[3][:, 0:128]
                    sS = ps[3][:, 128:256]
                    S_.add("dve", lambda e: e.tensor_scalar(out=lfU[:], in0=U, scalar1=lf[:, col:col + 1], scalar2=None, op0=ALU.mult),
                           reads=["lf", "cf"], writes=[("lfU", r)])
                    S_.add("pe", lambda e: e.matmul(bbS, ONES, lfU[:], start=True, stop=True),
                           reads=[("lfU", r), "cf"], writes=[("ps", 3)])
                    S_.add("act", lambda e: e.activation(out=ebc[:], in_=bbS, func=AF.Exp), reads=[("ps", 3)], writes=[("ebc", r)])
                    S_.add("dve", lambda e: e.scalar_tensor_tensor(
                        out=dta[:], in0=bbS, scalar=av[:, col:col + 1], in1=MB, op0=ALU.add, op1=ALU.add),
                        reads=[("ps", 3), "av", "cf"], writes=[("dta", r)])
                    S_.add("act", lambda e: e.activation(out=DT[:], in_=dta[:], func=AF.Exp), reads=[("dta", r)], writes=[("DT", r)])
                    for m in range(2):
                        S_.add("dve", lambda e, m=m: e.tensor_tensor(out=qhat[:, m, :], in0=qT[:, m, cs], in1=ebc[:], op=ALU.mult),
                               reads=["qT", ("ebc", r)], writes=[("qhat", rq)])

                    def smm(e):
                        for m in range(2):
                            ins = e.matmul(sS, kT[:, m, cs], qT[:, m, cs], start=(m == 0), stop=(m == 1))
                        return ins
                    S_.add("pe", smm, reads=["kT", "qT"], writes=[("ps", 3)])
                    S_.add("dve", lambda e: e.tensor_tensor(out=wT[:], in0=sS, in1=DT[:], op=ALU.mult),
                           reads=[("ps", 3), ("DT", r)], writes=[("wT", rq)])
                    return qhat, wT, rq

                def stage_M(tc, qhat, wT, rq, h=h):
                    col = tc * 4 + h
                    r = self.rot("rrM", 2)
                    hn, sm = L[f"hn{r}"], L[f"sm{r}"]
                    nbk = (4, 0)[r] if not os.environ.get('K_NOROT') else 4
                    psN = ps[nbk]

                    def nmm(e):
                        for m in range(2):
                            e.matmul(psN[:], qhat[:, m, :], Cbf[:, h, m, :], start=(m == 0), stop=False)
                        e.matmul(psN[:], wT[:], vtok[:, tc, :], start=False, stop=True)
                        for m in range(2):
                            e.matmul(ps[2][:, 64:65], qhat[:, m, :], nbf[:, h, m:m + 1], start=(m == 0), stop=False)
                        return e.matmul(ps[2][:, 64:65], wT[:], self.ones_col[:], start=False, stop=True)
                    S_.add("pe", nmm, reads=[("qhat", rq), ("wT", rq), ("Cbf", h), ("nbf", h), ("vtok", tc), "ones_col"],
                           writes=[("ps", nbk), ("ps", 2)])
                    S_.add("act", lambda e: e.activation(out=sm[:, 0:1], in_=ps[2][:, 64:65], func=AF.Square),
                           reads=[("ps", 2)], writes=[("sm", r)])

                    def cmm(e):
                        for m in range(2):
                            e.matmul(ps[5 + m][:], ktil[:, tc, m * 128:(m + 1) * 128], vtok[:, tc, :], start=True, stop=True)
                        for m in range(2):
                            ins = e.matmul(ps[2][:, 66 + m:67 + m], ktil[:, tc, m * 128:(m + 1) * 128], self.ones_col[:], start=True, stop=True)
                        return ins
                    S_.add("pe", cmm, reads=[("ktil", tc), ("vtok", tc), "ones_col"], writes=[("ps", 5), ("ps", 6), ("ps", 2)])
                    for m in range(2):
                        S_.add("dve", lambda e, m=m: e.scalar_tensor_tensor(
                            out=C[:, h, m, :], in0=C[:, h, m, :], scalar=dec[:, col:col + 1], in1=ps[5 + m][:], op0=ALU.mult, op1=ALU.add),
                            reads=[("C", h), "dec", ("ps", 5 + m)], writes=[("C", h)])
                        S_.add("act", lambda e, m=m: e.activation(out=Cbf[:, h, m, :], in_=C[:, h, m, :], func=AF.Copy),
                               reads=[("C", h)], writes=[("Cbf", h)])
                    S_.add("dve", lambda e: e.scalar_tensor_tensor(
                        out=nst[:, h, :], in0=nst[:, h, :], scalar=dec[:, col:col + 1], in1=ps[2][:, 66:68], op0=ALU.mult, op1=ALU.add),
                        reads=[("n", h), "dec", ("ps", 2)], writes=[("n", h)])
                    S_.add("act", lambda e: e.activation(out=nbf[:, h, :], in_=nst[:, h, :], func=AF.Copy), reads=[("n", h)], writes=[("nbf", h)])
                    S_.add("act", lambda e: e.activation(out=L["sq0"][:], in_=psN[:], func=AF.Square, accum_out=sm[:, 2:3]),
                           reads=[("ps", nbk)], writes=[("sq", 0), ("sm", r)])
                    S_.add("dve", lambda e: e.tensor_scalar(out=sm[:, 1:2], in0=sm[:, 0:1], scalar1=1.0, scalar2=RMS_EPS, op0=ALU.max, op1=ALU.mult),
                           reads=[("sm", r)], writes=[("sm", r)])
                    S_.add("dve", lambda e: e.scalar_tensor_tensor(out=sm[:, 3:4], in0=sm[:, 2:3], scalar=1.0 / M_DV, in1=sm[:, 1:2], op0=ALU.mult, op1=ALU.add),
                           reads=[("sm", r)], writes=[("sm", r)])
                    S_.add("act", lambda e: e.activation(out=sm[:, 3:4], in_=sm[:, 3:4], func=AF.Sqrt), reads=[("sm", r)], writes=[("sm", r)])
                    S_.add("dve", lambda e: e.reciprocal(out=sm[:, 4:5], in_=sm[:, 3:4]), reads=[("sm", r)], writes=[("sm", r)])
                    S_.add("act", lambda e: e.activation(out=hn[:], in_=psN[:], func=AF.Copy, scale=sm[:, 4:5]),
                           reads=[("ps", nbk), ("sm", r)], writes=[("hn", r)])
                    return hn, r

                def stage_T(tc, hn, r, h=h):
                    cs = slice(tc * 128, (tc + 1) * 128)

                    def htr(e):
                        for i in range(4):
                            ins = e.transpose(psb[:, 512 + i * 128:512 + (i + 1) * 128], hn[:, i * 128:(i + 1) * 128], IDN)
                        return ins
                    S_.add("pe", htr, reads=[("hn", r), "cb"], writes=[("ps", 7)])
                    for i in range(4):
                        S_.add("dve", lambda e, i=i: e.scalar_tensor_tensor(
                            out=hgT[:, h * 4 + i, cs], in0=psb[:, 512 + i * 128:512 + (i + 1) * 128],
                            scalar=L["gainT"][:, h * 4 + i:h * 4 + i + 1], in1=sgo[:, i, cs], op0=ALU.mult, op1=ALU.mult),
                            reads=[("ps", 7), "m_gain", "sgo"], writes=[("hgT", h)])

                pq_ = {}
                mq_ = {}
                if os.environ.get('K_SEQ'):
                    for tc in range(4):
                        stage_T(tc, *stage_M(tc, *stage_P(tc)))
                    continue
                order = os.environ.get('K_ORD', "P0,M0,T0,P1,M1,T1,P2,M2,T2,P3,M3,T3").split(",")
                for tok in order:
                    tc = int(tok[1])
                    if tok[0] == "P":
                        pq_[tc] = stage_P(tc)
                    elif tok[0] == "M":
                        mq_[tc] = stage_M(tc, *pq_.pop(tc))
                    else:
                        stage_T(tc, *mq_.pop(tc))
            self.out_proj(L, self.d_mwo, hgT, [("hgT", h) for h in range(H)], b, cscale)
            for th in self.layernorm(L, b, lnidx, eps, final, 5, 6):
                th()

    def attn(self, lnidx, final):
        S_ = self.S
        ps, psb = self.ps, self.psb
        H = D_HEADS
        nb = self.nblocks
        lam_init = 0.8 - 0.6 * math.exp(-0.3 * 1)
        common = [("aT", [128, NCH, S], BF16), ("gainT", [128, NCH], F32), ("gain2", [128, NCH], F32)]
        LA = self.layout("attnA", common
                         + [(f"w{i}", [128, NCH, 256], BF16) for i in range(3)]
                         + [("qT", [128, 2, S], BF16), ("kT", [128, 2, S], BF16), ("vaug", [128, 16, 258], BF16)]
                         + [(f"pT{i}", [128, 512], BF16) for i in range(3)]
                         + [("accS0", [128, 4, 258], F32), ("accS1", [128, 4, 258], F32), ("a32", [128, 256], F32)]
                         + [(f"an{i}", [128, 256], BF16) for i in range(4)]
                         + [(f"sm{i}", [128, 8], F32) for i in range(4)]
                         + [("alA", [3, H, 128], BF16), ("alB", [3, H, 512], BF16), ("alC", [128, H * 20], F32),
                            ("lamr", [128, 512], F32), ("lt", [128, 256], F32), ("ls", [128, 8], F32)])
        LB = self.layout("attnB", common
                         + [("zT", [128, NCH, TB], F32)]
                         + [(f"sq{i}", [128, TB], F32) for i in range(2)]
                         + [("mean", [128, TB], F32), ("rstd", [128, TB], F32), ("zsum", [128, TB], F32), ("sqsum", [128, TB], F32)]
                         + [(f"wo{i}", [128, NCH, 128], BF16) for i in range(2)])
        aT, qT, kT, vaug, a32 = LA["aT"], LA["qT"], LA["kT"], LA["vaug"], LA["a32"]
        alA, alB, alC, ls = LA["alA"], LA["alB"], LA["alC"], LA["ls"]
        IDN, TRI = self.cb[:, 0, :], self.cb[:, 1, :]
        cscale = 1.0 / ALPHA
        eps = LN_EPS / (ALPHA * ALPHA)
        S_.add("pool", lambda e: e.dma_start(out=alA[:], in_=self.d_alA), writes=["alA"], dma="pc0")
        S_.add("pool", lambda e: e.dma_start(out=alB[:], in_=self.d_alB), writes=["alB"], dma="pc1")
        S_.add("sp", lambda e: e.dma_start(out=alC[:], in_=self.d_alC), writes=["alC"], dma="sc2")
        S_.add("sp", lambda e: e.dma_start(out=LA["gainT"][:], in_=self.d_dgain), writes=["d_gain"], dma="sc3")
        S_.add("sp", lambda e: e.dma_start(out=LA["lamr"][:], in_=self.d_dlam), writes=["lamr"], dma="sc4")
        S_.add("dve", lambda e: e.tensor_scalar(out=LA["gain2"][:], in0=LA["gainT"][:], scalar1=1.0 - lam_init, scalar2=None, op0=ALU.mult),
               reads=["d_gain"], writes=["gain2"])
        S_.add("dve", lambda e: e.memset(vaug[:, :, 256:258], 1.0), writes=["vones"])
        S_.add("dve", lambda e: e.tensor_tensor(out=LA["lt"][:].rearrange("p (a b) -> p a b", b=128),
                                                in0=LA["lamr"][:].rearrange("p (a b c) -> p a b c", a=2, b=2)[:, :, 0, :],
                                                in1=LA["lamr"][:].rearrange("p (a b c) -> p a b c", a=2, b=2)[:, :, 1, :], op=ALU.mult),
               reads=["lamr"], writes=["lt"])
        S_.add("dve", lambda e: e.reduce_sum(out=ls[:, 0:2], in_=LA["lt"][:].rearrange("p (a b) -> p a b", b=128), axis=AX.X),
               reads=["lt"], writes=["ls"])
        S_.add("act", lambda e: e.activation(out=ls[:, 2:4], in_=ls[:, 0:2], func=AF.Exp), reads=["ls"], writes=["ls"])
        S_.add("dve", lambda e: e.tensor_tensor(out=ls[:, 4:5], in0=ls[:, 3:4], in1=ls[:, 2:3], op=ALU.subtract), reads=["ls"], writes=["ls"])
        S_.add("dve", lambda e: e.tensor_scalar(out=ls[:, 5:6], in0=ls[:, 4:5], scalar1=-lam_init, scalar2=None, op0=ALU.add), reads=["ls"], writes=["ls"])
        NLAM = ls[:, 5:6]

        def load_w(src):
            ws = self.rot("dw", 3)
            wt = LA[f"w{ws}"]
            S_.add("pool", lambda e: e.dma_start(out=wt[:], in_=src), writes=[("dw", ws)], dma=("w", ws))
            return wt, ("dw", ws)

        fin_pending = []
        for h in range(H):
            for which, dst, scale in ((0, qT, D_HDIM ** -0.5), (1, kT, 1.0)):
                wt, wkey = load_w(self.d_dw[h, which])
                for c in range(2):
                    for tb in range(nb):
                        pb = self.rot("pj", 2)

                        def mm(e, wt=wt, pb=pb, c=c, tb=tb):
                            for k in range(NCH):
                                ins = e.matmul(ps[pb][:], wt[:, k, c * 128:(c + 1) * 128], self.xT[:, k, tb * TB:(tb + 1) * TB],
                                               start=(k == 0), stop=(k == NCH - 1))
                            return ins
                        S_.add("pe", mm, reads=[wkey, ("xT", tb)], writes=[("ps", pb)])
                        S_.add("act", lambda e, pb=pb, c=c, tb=tb, dst=dst, scale=scale: e.activation(
                            out=dst[:, c, tb * TB:(tb + 1) * TB], in_=ps[pb][:], func=AF.Copy, scale=scale),
                            reads=[("ps", pb)], writes=[("qk", which)])
            wt, wkey = load_w(self.d_dw[h, 2])
            for tcn in range(4 * nb):
                pb = self.rot("pj", 2)

                def mmv(e, wt=wt, pb=pb, tcn=tcn):
                    for k in range(NCH):
                        ins = e.matmul(ps[pb][:, 0:256], self.xT[:, k, tcn * 128:(tcn + 1) * 128], wt[:, k, :],
                                       start=(k == 0), stop=(k == NCH - 1))
                    return ins
                S_.add("pe", mmv, reads=[wkey, ("xT", tcn // 4)], writes=[("ps", pb)])
                S_.add("dve", lambda e, pb=pb, tcn=tcn: e.tensor_copy(out=vaug[:, tcn, 0:256], in_=ps[pb][:, 0:256]),
                       reads=[("ps", pb)], writes=[("v", tcn)])
            for qb in range(nb):
                for c in range(2):
                    st1, st2 = [], []
                    for j in range(4 * qb + 4):
                        dj = j - 4 * qb
                        i0 = max(dj, 0)
                        ncols = 512 - 128 * i0
                        q0 = qb * 512 + i0 * 128
                        cidx = h * 20 + (dj + 12)

                        def stage1(j=j, dj=dj, i0=i0, ncols=ncols, q0=q0, cidx=cidx, c=c, h=h):
                            sbk = (6, 0, 1)[self.rot("sb", 3)]
                            pr = self.rot("pT", 3)
                            pT = LA[f"pT{pr}"]

                            def smm(e):
                                e.matmul(ps[sbk][:, 0:ncols], kT[:, c, j * 128:(j + 1) * 128], qT[:, c, q0:q0 + ncols], start=True, stop=False)
                                return e.matmul(ps[sbk][:, 0:ncols], alA[0:3, h, :], alB[0:3, h, i0 * 128:512], start=False, stop=True)
                            S_.add("pe", smm, reads=[("qk", 0), ("qk", 1), "alA", "alB"], writes=[("ps", sbk)])
                            S_.add("act", lambda e: e.activation(out=pT[:, 0:ncols], in_=ps[sbk][:, 0:ncols], func=AF.Exp,
                                                                 bias=alC[:, cidx:cidx + 1]),
                                   reads=[("ps", sbk), "alC"], writes=[("pT", pr)])
                            if dj >= 0:
                                S_.add("dve", lambda e: e.tensor_tensor(out=pT[:, 0:128], in0=pT[:, 0:128], in1=TRI, op=ALU.mult),
                                       reads=[("pT", pr), "cb"], writes=[("pT", pr)])
                            return pT, pr

                        def stage2(pT, pr, j=j, i0=i0, qb=qb):
                            def pv(e):
                                for i in range(i0, 4):
                                    ins = e.matmul(ps[2 + i][:, 0:257], pT[:, (i - i0) * 128:(i - i0 + 1) * 128], vaug[:, j, 0:257],
                                                   start=(j == 0), stop=(j == 4 * qb + i))
                                return ins
                            S_.add("pe", pv, reads=[("pT", pr), ("v", j), "vones"], writes=[("ps", 2 + i) for i in range(i0, 4)])
                        st1.append(stage1)
                        st2.append(stage2)
                    nj = len(st1)
                    SK = 2
                    held = {}
                    for t in range(nj + SK):
                        if t in (2, 4, 6, 8) and fin_pending:
                            fin_pending.pop(0)[1]()
                        if t < nj:
                            held[t] = st1[t]()
                        if t - SK >= 0:
                            st2[t - SK](*held.pop(t - SK))
                    while any(k in ("A", "B", "C") for k, _ in fin_pending):
                        fin_pending.pop(0)[1]()
                    accS = LA[f"accS{c}"]
                    for i in range(4):
                        acc = ps[2 + i]
                        if i % 2 == 0:
                            S_.add("act", lambda e, acc=acc, i=i, accS=accS: e.activation(out=accS[:, i, 0:257], in_=acc[:, 0:257], func=AF.Copy),
                                   reads=[("ps", 2 + i)], writes=[("accS", c, i)])
                        else:
                            S_.add("dve", lambda e, acc=acc, i=i, accS=accS: e.tensor_copy(out=accS[:, i, 0:257], in_=acc[:, 0:257]),
                                   reads=[("ps", 2 + i)], writes=[("accS", c, i)])
                    if c == 0:
                        continue

                    def combine(e_add_reads, i, sm, A0, A1):
                        S_.add("dve", lambda e: e.tensor_scalar(out=a32[:], in0=A0[:, i, 0:256], scalar1=sm[:, 0:1], scalar2=None, op0=ALU.mult),
                               reads=[("accS", 0, i), ("sm", i)], writes=["a32"])
                        S_.add("dve", lambda e: e.scalar_tensor_tensor(
                            out=a32[:], in0=A1[:, i, 0:256], scalar=sm[:, 1:2], in1=a32[:], op0=ALU.mult, op1=ALU.add),
                            reads=[("accS", 1, i), ("sm", i), "a32"], writes=["a32"])

                    def stage_A(qb=qb, h=h):
                        A0, A1 = LA["accS0"], LA["accS1"]
                        for i in range(4):
                            sm = LA[f"sm{i}"]
                            S_.add("dve", lambda e, sm=sm, i=i: e.reciprocal(out=sm[:, 0:1], in_=A0[:, i, 256:257]),
                                   reads=[("accS", 0, i)], writes=[("sm", i)])
                            S_.add("dve", lambda e, sm=sm, i=i: e.reciprocal(out=sm[:, 1:2], in_=A1[:, i, 256:257]),
                                   reads=[("accS", 1, i)], writes=[("sm", i)])
                            S_.add("dve", lambda e, sm=sm: e.tensor_tensor(out=sm[:, 1:2], in0=sm[:, 1:2], in1=NLAM, op=ALU.mult),
                                   reads=[("sm", i), "ls"], writes=[("sm", i)])
                            combine(None, i, sm, A0, A1)
                            S_.add("dve", lambda e: e.tensor_tensor(out=LA["lt"][:], in0=a32[:], in1=a32[:], op=ALU.mult),
                                   reads=["a32"], writes=["lt"])
                            S_.add("dve", lambda e, sm=sm: e.reduce_sum(out=sm[:, 2:3], in_=LA["lt"][:], axis=AX.X),
                                   reads=["lt"], writes=[("sm", i)])
                            S_.add("dve", lambda e, sm=sm: e.tensor_scalar(out=sm[:, 3:4], in0=sm[:, 2:3], scalar1=1.0 / (2 * D_HDIM), scalar2=RMS_EPS, op0=ALU.mult, op1=ALU.add),
                                   reads=[("sm", i)], writes=[("sm", i)])

                    def stage_B():
                        for i in range(4):
                            sm = LA[f"sm{i}"]
                            S_.add("act", lambda e, sm=sm: e.activation(out=sm[:, 5:6], in_=sm[:, 3:4], func=AF.Ln), reads=[("sm", i)], writes=[("smB", i)])
                            S_.add("act", lambda e, sm=sm: e.activation(out=sm[:, 4:5], in_=sm[:, 5:6], func=AF.Exp, scale=-0.5), reads=[("smB", i)], writes=[("smB", i)])

                    def stage_C():
                        A0, A1 = LA["accS0"], LA["accS1"]
                        for i in range(4):
                            sm, an = LA[f"sm{i}"], LA[f"an{i}"]
                            combine(None, i, sm, A0, A1)
                            S_.add("dve", lambda e, sm=sm, an=an: e.tensor_scalar(out=an[:], in0=a32[:], scalar1=sm[:, 4:5], scalar2=None, op0=ALU.mult),
                                   reads=["a32", ("smB", i)], writes=[("an", i)])

                    def finalize2(qb=qb, h=h):
                        for i in range(4):
                            an = LA[f"an{i}"]

                            def atr(e, an=an):
                                for ee in range(2):
                                    ins = e.transpose(psb[:, ee * 128:(ee + 1) * 128], an[:, ee * 128:(ee + 1) * 128], IDN)
                                return ins
                            S_.add("pe", atr, reads=[("an", i), "cb"], writes=[("ps", 7)])
                            qs = qb * 512 + i * 128
                            for ee in range(2):
                                S_.add("dve", lambda e, ee=ee, qs=qs: e.tensor_scalar(
                                    out=aT[:, h * 2 + ee, qs:qs + 128], in0=psb[:, ee * 128:(ee + 1) * 128],
                                    scalar1=LA["gain2"][:, h * 2 + ee:h * 2 + ee + 1], scalar2=None, op0=ALU.mult),
                                    reads=[("ps", 7), "gain2"], writes=[("aT", qb)])
                    fin_pending.extend([("A", stage_A), ("B", stage_B), ("C", stage_C), ("F", finalize2)])
        while fin_pending:
            fin_pending.pop(0)[1]()
        S_.barrier()
        pend = []
        for b in range(nb):
            tsl = slice(b * TB, (b + 1) * TB)
            self.out_proj(LB, self.d_dwo, aT[:, :, tsl], [("aT", b)], b, cscale, pend)
            pend = self.layernorm(LB, b, lnidx, eps, final, 5, 6)
            if final:
                for th in pend:
                    th()
                pend = []
        for th in pend:
            th()

    def out_proj(self, L, d_wo, srcT, src_keys, b, cscale, pend=()):
        S_ = self.S
        ps = self.ps
        zT = L["zT"]
        tsl = slice(b * TB, (b + 1) * TB)
        pend = list(pend)
        for j in range(NCH):
            if pend:
                pend.pop(0)()
            ws = self.rot("wo", 2)
            wt = L[f"wo{ws}"]
            S_.add("pool", lambda e, wt=wt, j=j: e.dma_start(out=wt[:], in_=d_wo[j]), writes=[("wo", ws)], dma=("wo", ws))
            pb = self.rot("pj", 2)

            def mm(e, wt=wt, pb=pb):
                for k in range(NCH):
                    ins = e.matmul(ps[pb][:], wt[:, k, :], srcT[:, k, :], start=(k == 0), stop=(k == NCH - 1))
                return ins
            S_.add("pe", mm, reads=[("wo", ws)] + list(src_keys), writes=[("ps", pb)])
            S_.add("dve", lambda e, pb=pb, j=j: e.scalar_tensor_tensor(
                out=zT[:, j, :], in0=ps[pb][:], scalar=cscale, in1=self.xT[:, j, tsl], op0=ALU.mult, op1=ALU.add),
                reads=[("ps", pb), ("xT", b)], writes=[("zT", j)])
            self.z_stats(L, j)

    def build(self):
        self.prologue()
        n = len(self.phases)
        for i, ph in enumerate(self.phases):
            final = (i == n - 1)
            if i > 0:
                self.S.barrier()
            if ph[0] == "ffn":
                self.ffn(ph[1], ph[2], final, x_from_input=(i == 0))
            elif ph[0] == "mlstm":
                self.mlstm(ph[1], final)
            elif ph[0] == "attn":
                self.attn(ph[1], final)
            else:
                raise ValueError(ph)
        self.S.emit(final_waits=self.out_dmas)
        return self.nc


ALL_PHASES = [("ffn", 0, 0), ("mlstm", 1), ("ffn", 1, 2), ("ffn", 2, 3), ("attn", 4), ("ffn", 3, 5)]


def _tile_w(w):
    C_ = w.shape[1]
    t = np.asarray(w, np.float32).reshape(NCH, 128, C_ // 256, 256)
    return np.ascontiguousarray(t.transpose(2, 1, 0, 3))


def _tile_wo(w):
    K_ = w.shape[0]
    t = np.asarray(w, np.float32).reshape(K_ // 128, 128, NCH, 128)
    return np.ascontiguousarray(t.transpose(2, 1, 0, 3))


def prep_inputs(inp, kinds=("ffn", "mlstm", "attn")):
    f = {}
    if "ffn" in kinds:
        wins, wouts = [], []
        for (wi, wo) in ((inp["ffn1_w_in"][0], inp["ffn1_w_out"][0]), (inp["ffn2_w_in"][0], inp["ffn2_w_out"][0]),
                         (inp["ffn1_w_in"][1], inp["ffn1_w_out"][1]), (inp["ffn2_w_in"][1], inp["ffn2_w_out"][1])):
            wi = np.asarray(wi, np.float32)
            g = wi[:, :DFF].reshape(NCH, 128, NFF, 128)
            u = wi[:, DFF:].reshape(NCH, 128, NFF, 128)
            gu = np.stack([g, u], axis=3)
            wins.append(np.ascontiguousarray(gu.transpose(2, 1, 0, 3, 4)).reshape(NFF, 128, NCH, 256))
            wouts.append(_tile_wo(wo))
        f["ffn_win"] = np.stack(wins)
        f["ffn_wout"] = np.stack(wouts)
    lg = np.asarray(inp["ln_gain"], np.float32).reshape(6, NCH, 128)
    lb = np.asarray(inp["ln_bias"], np.float32).reshape(6, NCH, 128)
    f["ln_g"] = np.ascontiguousarray(lg.transpose(2, 0, 1))
    f["ln_b"] = np.ascontiguousarray(lb.transpose(2, 0, 1))
    s_ = np.arange(128)[:, None]
    t_ = np.arange(128)[None, :]
    tri = (s_ <= t_)
    cf = np.zeros((128, 3, 128), np.float32)
    cf[:, 0, :] = tri
    cf[:, 1, :] = np.where(tri, 0.0, -30000.0)
    cf[:, 2, :] = 1.0
    f["c_f32"] = cf
    cbm = np.zeros((128, 2, 128), np.float32)
    cbm[:, 0, :] = np.eye(128)
    cbm[:, 1, :] = tri
    f["c_bf"] = cbm
    if "mlstm" in kinds:
        mw = np.asarray(inp["m_w_in"][0], np.float32)
        pieces = []
        for h in range(M_HEADS):
            cols = np.concatenate([np.arange(h * 256, (h + 1) * 256), 1024 + np.arange(h * 256, (h + 1) * 256),
                                   2048 + np.arange(h * 512, (h + 1) * 512), 4096 + np.arange(h * 512, (h + 1) * 512)])
            pieces.append(_tile_w(mw[:, cols]))
        f["m_w"] = np.stack(pieces)
        f["m_wif"] = np.ascontiguousarray(mw[:, 6144:6152].reshape(NCH, 128, 8).transpose(1, 0, 2))
        f["m_bg"] = np.ascontiguousarray(np.broadcast_to(np.tile(np.asarray(inp["m_b_gates"][0], np.float32), 4)[None, :], (128, 32)))
        f["m_gain"] = np.ascontiguousarray(np.asarray(inp["m_norm_gain"][0], np.float32).reshape(NCH, 128).T)
        f["m_wo"] = _tile_wo(inp["m_w_out"][0])
    if "attn" in kinds:
        dw = np.asarray(inp["d_w_in"][0], np.float32)
        pieces = []
        for h in range(D_HEADS):
            cols = np.concatenate([np.arange(h * 256, (h + 1) * 256), 2048 + np.arange(h * 256, (h + 1) * 256),
                                   4096 + np.arange(h * 256, (h + 1) * 256)])
            pieces.append(_tile_w(dw[:, cols]))
        f["d_w"] = np.stack(pieces)
        f["d_wo"] = _tile_wo(inp["d_w_out"][0])
        f["d_gain"] = np.ascontiguousarray(np.asarray(inp["d_norm_gain"][0], np.float32).reshape(NCH, 128).T)
        f["d_lam"] = np.ascontiguousarray(np.broadcast_to(np.asarray(inp["d_lambda"][0], np.float32).reshape(1, 512), (128, 512)))
        slopes = np.array([2.0 ** (-8.0 * (h + 1) / D_HEADS) for h in range(D_HEADS)], np.float32)
        alA = np.ones((3, D_HEADS, 128), np.float32)
        alA[0] = slopes[:, None] * np.arange(128, dtype=np.float32)[None, :]
        alB = np.ones((3, D_HEADS, 512), np.float32)
        qrel = np.arange(512)
        alB[1] = -slopes[:, None] * (qrel % 128).astype(np.float32)[None, :]
        alB[2] = -slopes[:, None] * (128.0 * (qrel // 128)).astype(np.float32)[None, :]
        alC = np.zeros((128, D_HEADS * 20), np.float32)
        for h in range(D_HEADS):
            for dj in range(-12, 4):
                alC[:, h * 20 + dj + 12] = slopes[h] * 128.0 * dj
        f["al_A"], f["al_B"], f["al_C"] = alA, alB, alC
    return f


def x_to_dev(xb):
    return np.ascontiguousarray(np.asarray(xb, np.float32).T.reshape(NCH, 128, S).transpose(1, 0, 2))


def y_from_dev(yT):
    return np.ascontiguousarray(yT.transpose(1, 0, 2).reshape(D, S).T)


def run(inp, phases, n_cores=8, trace=False):
    kinds = {p[0] for p in phases}
    shared = prep_inputs(inp, kinds)
    nc = Builder(phases).build()
    x = inp["x"]
    in_maps = []
    for c in range(n_cores):
        m = dict(shared)
        m["xT_in"] = x_to_dev(x[c])
        in_maps.append(m)
    res = run_bass_kernel_spmd(nc, in_maps, core_ids=list(range(n_cores)), trace=trace)
    out = np.stack([y_from_dev(np.asarray(r["yT_out"])) for r in res.results])
    return out, res


def kernel(**inputs):
    out, _ = run(inputs, ALL_PHASES, 8)
    return out.astype(np.float32)
```
